# Optimizing a Trainium2 kernel written in Bass

```python
import jax, jax.numpy as jnp
from jax import lax
import numpy as np

D_MODEL = 1024
BATCH = 4
SEQ = 8192
DEPTH = 2

N_META = 16
CONV_GROUPS = 8
CONV_GROUP_DIM = 64
D_CONV = CONV_GROUPS * CONV_GROUP_DIM
CONV_WIDTH = 3
MLA_HEADS = 8
QK_NOPE = 64
QK_ROPE = 32
V_HEAD = 64
Q_LORA = 256
KV_LORA = 128
D_MLA = MLA_HEADS * V_HEAD
ROPE_BASE = 10000.0
Q_BLOCK = 128
NEG_INF = -1e30
D_FF = 2816
N_BRANCH = 2
ALPHA = (2 * DEPTH) ** 0.25
BETA = (8 * DEPTH) ** -0.25
LN_EPS = 1e-5
RMS_EPS = 1e-6
IN_SPLITS = (D_CONV, D_CONV, D_CONV, Q_LORA, KV_LORA, QK_ROPE, D_MODEL, D_MODEL)
D_IN = sum(IN_SPLITS)

kernel_name = 'hybrid_shortconv_mla_macaron_deepnorm'


def layer_norm(x, g, b):
    xf = x.astype(jnp.float32)
    mu = jnp.mean(xf, axis=-1, keepdims=True)
    var = jnp.mean(jnp.square(xf - mu), axis=-1, keepdims=True)
    return ((xf - mu) * lax.rsqrt(var + LN_EPS) * g + b).astype(x.dtype)


def rms_norm(x, g):
    xf = x.astype(jnp.float32)
    return (xf * lax.rsqrt(jnp.mean(jnp.square(xf), axis=-1, keepdims=True) + RMS_EPS) * g).astype(x.dtype)


def rope_tables(T):
    inv_freq = 1.0 / (ROPE_BASE ** (jnp.arange(0, QK_ROPE, 2, dtype=jnp.float32) / QK_ROPE))
    ang = jnp.arange(T, dtype=jnp.float32)[:, None] * inv_freq[None, :]
    return jnp.cos(ang), jnp.sin(ang)


def apply_rope(x, cos, sin):
    x1, x2 = jnp.split(x.astype(jnp.float32), 2, axis=-1)
    return jnp.concatenate([x1 * cos - x2 * sin, x2 * cos + x1 * sin], axis=-1).astype(x.dtype)


def swiglu(x, w_up, w_down):
    gate, up = jnp.split(x @ w_up, 2, axis=-1)
    return (jax.nn.silu(gate) * up) @ w_down


def causal_short_conv(u, w):
    T = u.shape[1]
    up = jnp.pad(u, ((0, 0), (CONV_WIDTH - 1, 0), (0, 0)))
    out = up[:, 0:T] * w[0]
    for k in range(1, CONV_WIDTH):
        out = out + up[:, k:k + T] * w[k]
    return out


def mla_causal_attention(q_nope, q_rope, k_nope, k_rope, v):
    bsz, T = q_nope.shape[:2]
    n_blocks = -(-T // Q_BLOCK)
    pad = n_blocks * Q_BLOCK - T
    scale = (QK_NOPE + QK_ROPE) ** -0.5

    def to_blocks(a):
        a = jnp.pad(a, ((0, 0), (0, pad), (0, 0), (0, 0)))
        return jnp.moveaxis(a.reshape(bsz, n_blocks, Q_BLOCK, *a.shape[2:]), 1, 0)

    q_pos = jnp.arange(n_blocks * Q_BLOCK, dtype=jnp.int32).reshape(n_blocks, Q_BLOCK)
    k_pos = jnp.arange(T, dtype=jnp.int32)

    def one_block(args):
        qn, qr, qp = args
        s = (jnp.einsum('bqhd,bkhd->bhqk', qn, k_nope)
             + jnp.einsum('bqhr,bkr->bhqk', qr, k_rope)).astype(jnp.float32) * scale
        s = jnp.where(k_pos[None, :] <= qp[:, None], s, NEG_INF)
        p = jax.nn.softmax(s, axis=-1).astype(v.dtype)
        return jnp.einsum('bhqk,bkhd->bqhd', p, v)

    out = lax.map(one_block, (to_blocks(q_nope), to_blocks(q_rope), q_pos))
    out = jnp.moveaxis(out, 0, 1).reshape(bsz, n_blocks * Q_BLOCK, MLA_HEADS, V_HEAD)
    return out[:, :T]


def hybrid_mixer(x, w_in, b_gate, conv_w, q_norm_g, w_uq, kv_norm_g, w_ukv, w_br_conv, w_br_mla, w_o, cos, sin):
    bsz, T, _ = x.shape
    cuts = [int(c) for c in np.cumsum(IN_SPLITS)[:-1]]
    b_in, c_in, h_in, c_q, c_kv, k_r, g_conv, g_mla = jnp.split(x @ w_in, cuts, axis=-1)
    y_conv = b_in * causal_short_conv(c_in * h_in, conv_w)
    q = (rms_norm(c_q, q_norm_g) @ w_uq).reshape(bsz, T, MLA_HEADS, QK_NOPE + QK_ROPE)
    q_nope = q[..., :QK_NOPE]
    q_rope = apply_rope(q[..., QK_NOPE:], cos[None, :, None], sin[None, :, None])
    kv = (rms_norm(c_kv, kv_norm_g) @ w_ukv).reshape(bsz, T, MLA_HEADS, QK_NOPE + V_HEAD)
    k_nope, v = kv[..., :QK_NOPE], kv[..., QK_NOPE:]
    k_rope = apply_rope(k_r, cos[None], sin[None])
    y_mla = mla_causal_attention(q_nope, q_rope, k_nope, k_rope, v).reshape(bsz, T, D_MLA)
    merged = (jax.nn.sigmoid(g_conv + b_gate[0]) * (y_conv @ w_br_conv)
              + jax.nn.sigmoid(g_mla + b_gate[1]) * (y_mla @ w_br_mla))
    return merged @ w_o


def setup_inputs(seed: int = 0) -> dict:
    key = jax.random.key(seed)
    ks = jax.random.split(key, 18)
    L = DEPTH

    def dense(k, shape, scale=1.0):
        return jax.random.normal(k, shape, jnp.float32) * (scale * shape[-2] ** -0.5)

    def near_one(k, shape):
        return 1.0 + 0.02 * jax.random.normal(k, shape, jnp.float32)

    def small(k, shape):
        return 0.02 * jax.random.normal(k, shape, jnp.float32)

    return {
        'x': jax.random.normal(ks[0], (BATCH, SEQ, D_MODEL), jnp.float32),
        'meta_tokens': jax.random.normal(ks[1], (N_META, D_MODEL), jnp.float32),
        'ffn1_w_up': dense(ks[2], (L, D_MODEL, 2 * D_FF)),
        'ffn1_w_down': dense(ks[3], (L, D_FF, D_MODEL), BETA),
        'mix_w_in': dense(ks[4], (L, D_MODEL, D_IN)),
        'mix_b_gate': small(ks[5], (L, N_BRANCH, D_MODEL)),
        'conv_w': dense(ks[6], (L, CONV_WIDTH, D_CONV)),
        'q_norm_g': near_one(ks[7], (L, Q_LORA)),
        'w_uq': dense(ks[8], (L, Q_LORA, MLA_HEADS * (QK_NOPE + QK_ROPE))),
        'kv_norm_g': near_one(ks[9], (L, KV_LORA)),
        'w_ukv': dense(ks[10], (L, KV_LORA, MLA_HEADS * (QK_NOPE + V_HEAD))),
        'w_br_conv': dense(ks[11], (L, D_CONV, D_MODEL)),
        'w_br_mla': dense(ks[12], (L, D_MLA, D_MODEL)),
        'w_o': dense(ks[13], (L, D_MODEL, D_MODEL), BETA),
        'ffn2_w_up': dense(ks[14], (L, D_MODEL, 2 * D_FF)),
        'ffn2_w_down': dense(ks[15], (L, D_FF, D_MODEL), BETA),
        'ln_g': near_one(ks[16], (L, 3, D_MODEL)),
        'ln_b': small(ks[17], (L, 3, D_MODEL)),
    }


def reference(x, meta_tokens, ffn1_w_up, ffn1_w_down, mix_w_in, mix_b_gate, conv_w, q_norm_g, w_uq,
              kv_norm_g, w_ukv, w_br_conv, w_br_mla, w_o, ffn2_w_up, ffn2_w_down, ln_g, ln_b):
    bsz = x.shape[0]
    meta = jnp.broadcast_to(meta_tokens[None].astype(x.dtype), (bsz, N_META, D_MODEL))
    h = jnp.concatenate([meta, x], axis=1)
    cos, sin = rope_tables(h.shape[1])
    for l in range(DEPTH):
        h = layer_norm(ALPHA * h + 0.5 * swiglu(h, ffn1_w_up[l], ffn1_w_down[l]), ln_g[l, 0], ln_b[l, 0])
        mix = hybrid_mixer(h, mix_w_in[l], mix_b_gate[l], conv_w[l], q_norm_g[l], w_uq[l], kv_norm_g[l],
                           w_ukv[l], w_br_conv[l], w_br_mla[l], w_o[l], cos, sin)
        h = layer_norm(ALPHA * h + mix, ln_g[l, 1], ln_b[l, 1])
        h = layer_norm(ALPHA * h + 0.5 * swiglu(h, ffn2_w_up[l], ffn2_w_down[l]), ln_g[l, 2], ln_b[l, 2])
    return h[:, N_META:]
```

```python
import contextlib
import numpy as np
import ml_dtypes
import concourse.bass as bass
import concourse.mybir as mybir
from concourse.bass_utils import run_bass_kernel_spmd

F32 = mybir.dt.float32
BF16 = mybir.dt.bfloat16
AF = mybir.ActivationFunctionType
ALU = mybir.AluOpType

D = 1024
DFF = 2816
NMETA = 16
DIN = 4000
NH = 8
ALPHA = 4.0 ** 0.25
LN_EPS = 1e-5
RMS_EPS = 1e-6
SCALE = 96.0 ** -0.5
TN = 512
NSLOT = 7


class Sem:
    def __init__(self, h):
        self.h = h
        self.count = 0


class Region:
    __slots__ = ("lw", "rd")

    def __init__(self):
        self.lw = None
        self.rd = {}


class Buf:
    def __init__(self, nreg=1):
        self.regs = [Region() for _ in range(nreg)]

    def r(self, k=0):
        return [self.regs[k]]

    def rs(self, a=None, b=None):
        return self.regs[a:b]


class Eng:
    def __init__(self, name, sem):
        self.name = name
        self.sem = sem
        self.items = []
        self.known = {}


class Sched:
    def __init__(self):
        self.engs = {}

    def op(self, en, fn, reads=(), writes=(), signal=True, dsem=None):
        eng = self.engs[en]
        deps = {}

        def add(d):
            if d is not None:
                s, v = d
                if deps.get(s, 0) < v:
                    deps[s] = v

        for r in reads:
            add(r.lw)
        for w in writes:
            add(w.lw)
            for s, v in w.rd.items():
                add((s, v))
        for s, v in deps.items():
            if en == "pe" and s is eng.sem:
                continue
            if eng.known.get(s, 0) >= v:
                continue
            eng.items.append(("w", s, v))
            eng.known[s] = v
        if dsem is not None:
            dsem.count += 16
            tick = (dsem, dsem.count)
            eng.items.append(("d", fn, dsem))
        elif signal:
            eng.sem.count += 1
            tick = (eng.sem, eng.sem.count)
            eng.items.append(("o", fn, True))
        else:
            tick = (eng.sem, eng.sem.count + 1)
            eng.items.append(("o", fn, False))
        for r in reads:
            if r.rd.get(tick[0], 0) < tick[1]:
                r.rd[tick[0]] = tick[1]
        for w in writes:
            w.lw = tick
            w.rd = {}

    def wait_all(self, en, sems):
        eng = self.engs[en]
        for s in sems:
            if s.count > 0:
                eng.items.append(("w", s, s.count))

    def replay(self, en, e):
        for it in self.engs[en].items:
            if it[0] == "w":
                e.wait_ge(it[1].h, it[2])
            elif it[0] == "d":
                it[1](e).then_inc(it[2].h, 16)
            else:
                ins = it[1](e)
                if it[2]:
                    ins.then_inc(self.engs[en].sem.h, 1)


def build_program(NT, L=2):
    SEQ = NT * TN
    nc = bass.Bass("TRN2", target_bir_lowering=False)
    S = Sched()
    es = contextlib.ExitStack()

    def dram(name, shape, dt, kind="Internal"):
        return nc.dram_tensor(name, list(shape), dt, kind=kind)

    xT = dram("xT", [D, SEQ], F32, "ExternalInput").ap()
    metaT = dram("metaT", [D, NMETA], F32, "ExternalInput").ap()
    w_ext = {}
    for nm, shp in (("ffn1_w_up", [L, D, 2 * DFF]), ("ffn1_w_down", [L, DFF, D]), ("mix_w_in", [L, D, DIN]),
                    ("w_uq", [L, 256, 768]), ("w_ukv", [L, 128, 1024]), ("w_br_conv", [L, 512, D]),
                    ("w_br_mla", [L, 512, D]), ("w_o", [L, D, D]), ("ffn2_w_up", [L, D, 2 * DFF]),
                    ("ffn2_w_down", [L, DFF, D])):
        w_ext[nm] = dram(nm, shp, F32, "ExternalInput").ap()
    p_lng = dram("p_lng", [128, L * 3 * 8], F32, "ExternalInput").ap()
    p_lnb = dram("p_lnb", [128, L * 3 * 8], F32, "ExternalInput").ap()
    p_bg = dram("p_bg", [128, L * 2 * 8], F32, "ExternalInput").ap()
    p_cw = dram("p_cw", [128, L * 3 * 4], F32, "ExternalInput").ap()
    p_gq = dram("p_gq", [128, L * 2], F32, "ExternalInput").ap()
    p_gkv = dram("p_gkv", [128, L], F32, "ExternalInput").ap()
    ropeC = dram("ropeC", [128, NMETA + SEQ], F32, "ExternalInput").ap()
    ropeS = dram("ropeS", [128, NMETA + SEQ], F32, "ExternalInput").ap()
    tri_d = dram("tri", [128, 128], BF16, "ExternalInput").ap()
    outT = dram("outT", [D, SEQ], F32, "ExternalOutput").ap()

    wb = {}
    for l in range(L):
        for f in (1, 2):
            wb["up", f, l] = dram(f"b_up{f}_{l}", [D, 2 * DFF], BF16).ap()
            wb["dn", f, l] = dram(f"b_dn{f}_{l}", [DFF, D], BF16).ap()
        wb["in", l] = dram(f"b_in_{l}", [D, DIN], BF16).ap()
        wb["kr", l] = dram(f"b_kr_{l}", [D, 2, 96], BF16).ap()
        wb["uqN", l] = dram(f"b_uqN_{l}", [256, 4, 128], BF16).ap()
        wb["uqR", l] = dram(f"b_uqR_{l}", [256, 2, 2, 128], BF16).ap()
        wb["ukv", l] = dram(f"b_ukv_{l}", [128, 1024], BF16).ap()
        wb["brc", l] = dram(f"b_brc_{l}", [512, D], BF16).ap()
        wb["brm", l] = dram(f"b_brm_{l}", [512, D], BF16).ap()
        wb["wo", l] = dram(f"b_wo_{l}", [D, D], BF16).ap()
    KTd = dram("KTd", [L, 96, NH, SEQ], BF16).ap()
    Vd = dram("Vd", [L, NT, 128, 4 * NH * 128], BF16).ap()
    wbuf = {k: Buf() for k in wb}
    KTd_b = Buf(L * NT)
    Vd_b = Buf(L * NT)

    def sb(name, shape, dt):
        return es.enter_context(nc.sbuf_tensor(name, list(shape), dt))

    ring_t = [sb(f"ring{i}", [128, 4096], BF16) for i in range(NSLOT)]
    ring_b = [Buf() for _ in range(NSLOT)]
    hf = sb("hf", [128, 8, TN], F32); hf_b = Buf(8)
    hb = sb("hb", [128, 8, TN], BF16); hb_b = Buf(8)
    rr = sb("rr", [128, 8, TN], F32); rr_b = Buf(8)
    big = sb("big", [128, 22, TN], BF16); big_b = Buf(22)
    rb = sb("rb", [128, 8, TN], BF16); rb_b = Buf(8)
    tmpA = sb("tmpA", [128, 2, TN], F32); tmpA_b = Buf(2)
    tmpG = sb("tmpG", [128, 2, TN], F32); tmpG_b = Buf(2)
    tmpM = sb("tmpM", [128, 2, TN], F32); tmpM_b = Buf(2)
    sqb = sb("sqb", [128, 8, TN], BF16); sqb_b = Buf(8)
    rbl = sb("rbl", [128, 8, TN], BF16); rbl_b = Buf(8)
    lnm = sb("lnm", [128, TN], F32); lnm_b = Buf()
    lnq = sb("lnq", [128, TN], F32); lnq_b = Buf()
    rstd = sb("rstd", [128, TN], F32); rstd_b = Buf()
    sdv = sb("sdv", [128, TN], F32); sdv_b = Buf()
    ub = sb("ub", [128, 4, TN + 2], F32); ub_b = Buf(4)
    ycb = sb("ycb", [128, 4, TN], BF16); ycb_b = Buf(4)
    cqn = sb("cqn", [128, 2, TN], BF16); cqn_b = Buf(2)
    Vt = sb("Vt", [128, 4, NH, 128], BF16); Vt_b = Buf(4)
    tabC = sb("tabC", [128, TN], F32); tabS = sb("tabS", [128, TN], F32); tab_b = Buf()
    PT = sb("PT", [128, 4, TN], BF16); PT_b = Buf(4)
    rec = sb("rec", [128, TN], F32); rec_b = Buf()
    cc = sb("cc", [128, L, 4, 2], F32); cc_b = Buf(L)
    mKT = sb("mKT", [128, L, NH, NMETA], BF16); mKT_b = Buf(L)
    mV = sb("mV", [128, L, NH, 128], BF16); mV_b = Buf(L)
    lng = sb("lng", [128, L * 3 * 8], F32); lnb = sb("lnb", [128, L * 3 * 8], F32)
    bgt = sb("bgt", [128, L * 2 * 8], F32); cwt = sb("cwt", [128, L * 3 * 4], F32)
    gqt = sb("gqt", [128, L * 2], F32); gkvt = sb("gkvt", [128, L], F32)
    ones = sb("ones", [128, 128], BF16); tri = sb("tri_s", [128, 128], BF16)
    par_b = Buf()
    ps = es.enter_context(nc.psum_tensor("ps", [128, 8, TN], F32)); ps_b = Buf(8)
    print("SBUF bytes remaining per partition:", nc.sbuf_bytes_remaining)

    def qT(h): return big[:, h, :]
    def KT(h): return big[:, 8 + h, :]
    def ymla(c): return big[:, 16 + c, :]
    ckvn = big[:, 20, :]
    csb = rr

    def sem(name):
        return Sem(es.enter_context(nc.semaphore(name)))

    for en in ("pe", "act", "dve", "pool", "sp"):
        S.engs[en] = Eng(en, sem("s_" + en))
    ring_sem = [sem(f"s_ring{i}") for i in range(NSLOT)]
    sem_par = sem("s_par")
    sem_x = sem("s_x")
    sem_tab = sem("s_tab")
    sem_kst = sem("s_kst")
    sem_vst = sem("s_vst")
    sem_out = sem("s_out")

    def dma(q, out, in_, reads, writes, dsem):
        S.op(q, lambda e: e.dma_start(out=out, in_=in_), reads, writes, dsem=dsem)

    def mm(out, lhsT, rhs, start, stop, reads, writes, signal=None):
        if signal is None:
            signal = stop
        S.op("pe", lambda e: e.matmul(out, lhsT=lhsT, rhs=rhs, start=start, stop=stop), reads, writes, signal=signal)

    def act_fn(out, in_, func, reads, writes, scale=1.0, bias=0.0):
        S.op("act", lambda e: e.activation(out=out, in_=in_, func=func, bias=bias, scale=scale), reads, writes)

    def act_copy(out, in_, reads, writes):
        S.op("act", lambda e: e.copy(out=out, in_=in_), reads, writes)

    def dve_tt(out, in0, in1, op, reads, writes):
        S.op("dve", lambda e: e.tensor_tensor(out=out, in0=in0, in1=in1, op=op), reads, writes)

    def dve_stt(out, in0, scalar, in1, op0, op1, reads, writes):
        S.op("dve", lambda e: e.scalar_tensor_tensor(out=out, in0=in0, scalar=scalar, in1=in1, op0=op0, op1=op1),
             reads, writes)

    def dve_ts(out, in0, scalar1, op0, reads, writes):
        S.op("dve", lambda e: e.tensor_scalar(out=out, in0=in0, scalar1=scalar1, scalar2=None, op0=op0), reads, writes)

    def pool_tt(out, in0, in1, op, reads, writes):
        S.op("pool", lambda e: e.tensor_tensor(out=out, in0=in0, in1=in1, op=op), reads, writes)

    def pool_stt(out, in0, scalar, in1, op0, op1, reads, writes):
        S.op("pool", lambda e: e.scalar_tensor_tensor(out=out, in0=in0, scalar=scalar, in1=in1, op0=op0, op1=op1),
             reads, writes)

    def pool_ts(out, in0, scalar1, op0, reads, writes):
        S.op("pool", lambda e: e.tensor_scalar(out=out, in0=in0, scalar1=scalar1, scalar2=None, op0=op0), reads, writes)

    def pool_copy(out, in_, reads, writes):
        S.op("pool", lambda e: e.tensor_copy(out=out, in_=in_), reads, writes)

    def dve_copy(out, in_, reads, writes):
        S.op("dve", lambda e: e.tensor_copy(out=out, in_=in_), reads, writes)

    def dve_recip(out, in_, reads, writes):
        S.op("dve", lambda e: e.reciprocal(out=out, in_=in_), reads, writes)

    psn = [0]

    def nbank():
        b = psn[0]
        psn[0] = (b + 1) % 6
        return b

    rn = [0]

    def ring_get(loads, src_bufs):
        s = rn[0]
        rn[0] = (s + 1) % NSLOT
        reads = [r for b in src_bufs for r in b]
        for (vf, src) in loads:
            dma("sp", vf(ring_t[s]), src, reads, ring_b[s].r(), ring_sem[s])
        return s

    for (t, d_) in ((lng, p_lng), (lnb, p_lnb), (bgt, p_bg), (cwt, p_cw), (gqt, p_gq), (gkvt, p_gkv), (tri, tri_d)):
        dma("pool", t[:, :], d_[:, :], [], par_b.r(), sem_par)
    S.op("dve", lambda e: e.memset(ones[:, :], 1.0), [], par_b.r())
    S.op("dve", lambda e: e.memset(cc[:, :, :, :], 0.0), [], cc_b.rs())
    for blk in range(4):
        S.op("dve", lambda e, blk=blk: e.memset(Vt[:, blk, :, 64:128], 1.0), [], Vt_b.r(blk))
    for l in range(L):
        S.op("dve", lambda e, l=l: e.memset(mV[:, l, :, 64:128], 1.0), [], mV_b.r(l))

    cast_sems = {}

    def cast(dst_key, dst_ap, src_ap):
        if dst_key not in cast_sems:
            cast_sems[dst_key] = sem("s_c_" + "_".join(str(x) for x in dst_key))
        dma("pool", dst_ap, src_ap, [], wbuf[dst_key].r(), cast_sems[dst_key])

    def cast_rows(dst_key, src, rows, blk=256):
        for r0 in range(0, rows, blk):
            cast(dst_key, wb[dst_key][r0:r0 + blk, :], src[r0:r0 + blk, :])

    for l in range(L):
        cast_rows(("up", 1, l), w_ext["ffn1_w_up"][l], D)
        cast_rows(("dn", 1, l), w_ext["ffn1_w_down"][l], DFF)
        win = w_ext["mix_w_in"][l]
        cast_rows(("in", l), win, D)
        cast(("kr", l), wb["kr", l][:, 0, :], win[:, 1856:1952])
        cast(("kr", l), wb["kr", l][:, 1, 0:64], win[:, 1856:1920])
        cast(("kr", l), wb["kr", l][:, 1, 64:80], win[:, 1936:1952])
        cast(("kr", l), wb["kr", l][:, 1, 80:96], win[:, 1920:1936])
        wq = w_ext["w_uq"][l].rearrange("k (h c) -> k h c", h=NH)
        for h in range(NH):
            g_, hh = h // 4, h % 4
            cast(("uqN", l), wb["uqN", l][:, h // 2, (h % 2) * 64:(h % 2) * 64 + 64], wq[:, h, 0:64])
            cast(("uqR", l), wb["uqR", l][:, 0, g_, hh * 32:hh * 32 + 32], wq[:, h, 64:96])
            cast(("uqR", l), wb["uqR", l][:, 1, g_, hh * 32:hh * 32 + 16], wq[:, h, 80:96])
            cast(("uqR", l), wb["uqR", l][:, 1, g_, hh * 32 + 16:hh * 32 + 32], wq[:, h, 64:80])
        cast(("ukv", l), wb["ukv", l][:, :], w_ext["w_ukv"][l])
        cast_rows(("brc", l), w_ext["w_br_conv"][l], 512)
        cast_rows(("brm", l), w_ext["w_br_mla"][l], 512)
        cast_rows(("wo", l), w_ext["w_o"][l], D)
        cast_rows(("up", 2, l), w_ext["ffn2_w_up"][l], D)
        cast_rows(("dn", 2, l), w_ext["ffn2_w_down"][l], DFF)

    ln_pend = []

    def ln_feed(dc, n):
        act_copy(rbl[:, dc, :n], rr[:, dc, :n], rr_b.r(dc), rbl_b.r(dc))
        act_fn(sqb[:, dc, :n], rr[:, dc, :n], AF.Square, rr_b.r(dc), sqb_b.r(dc))
        ln_pend.append(dc)

    def ln_flush(n):
        for dc in ln_pend:
            mm(ps[:, 6, :n], ones[:, :], rbl[:, dc, :n], dc == 0, dc == 7, rbl_b.r(dc) + par_b.r(), ps_b.r(6))
            mm(ps[:, 7, :n], ones[:, :], sqb[:, dc, :n], dc == 0, dc == 7, sqb_b.r(dc) + par_b.r(), ps_b.r(7))
        ln_pend.clear()

    def layernorm(l, i, n):
        ln_flush(n)
        S.op("dve", lambda e: e.tensor_scalar(out=lnm[:, :n], in0=ps[:, 6, :n], scalar1=1.0 / D, scalar2=None,
                                              op0=ALU.mult), ps_b.r(6), lnm_b.r())
        S.op("dve", lambda e: e.tensor_scalar(out=lnq[:, :n], in0=ps[:, 7, :n], scalar1=1.0 / D, scalar2=LN_EPS,
                                              op0=ALU.mult, op1=ALU.add), ps_b.r(7), lnq_b.r())
        dve_tt(sdv[:, :n], lnm[:, :n], lnm[:, :n], ALU.mult, lnm_b.r(), sdv_b.r())
        dve_tt(lnq[:, :n], lnq[:, :n], sdv[:, :n], ALU.subtract, lnq_b.r() + sdv_b.r(), lnq_b.r())
        act_fn(sdv[:, :n], lnq[:, :n], AF.Sqrt, lnq_b.r(), sdv_b.r())
        dve_recip(rstd[:, :n], sdv[:, :n], sdv_b.r(), rstd_b.r())
        dve_stt(lnm[:, :n], lnm[:, :n], -1.0, rstd[:, :n], ALU.mult, ALU.mult, lnm_b.r() + rstd_b.r(), lnm_b.r())
        for dc in range(8):
            gi = (l * 3 + i) * 8 + dc
            dve_tt(rr[:, dc, :n], rr[:, dc, :n], rstd[:, :n], ALU.mult, rr_b.r(dc) + rstd_b.r(), rr_b.r(dc))
            pool_tt(rr[:, dc, :n], rr[:, dc, :n], lnm[:, :n], ALU.add, rr_b.r(dc) + lnm_b.r(), rr_b.r(dc))
            act_fn(hb[:, dc, :n], rr[:, dc, :n], AF.Identity, rr_b.r(dc) + par_b.r(), hb_b.r(dc),
                   scale=lng[:, gi:gi + 1], bias=lnb[:, gi:gi + 1])
            act_fn(hf[:, dc, :n], rr[:, dc, :n], AF.Identity, rr_b.r(dc) + par_b.r(), hf_b.r(dc),
                   scale=lng[:, gi:gi + 1], bias=lnb[:, gi:gi + 1])

    def ffn(l, f, n):
        wu = wb["up", f, l].rearrange("(kc p) c -> p kc c", p=128)
        wd = wb["dn", f, l].rearrange("(fc p) d -> p fc d", p=128)
        for u in range(11):
            s = ring_get([(lambda t: t[:, 0:4096].rearrange("p (k g c) -> p k g c", k=8, g=2)[:, :, 0, :],
                           wu[:, :, 256 * u:256 * u + 256]),
                          (lambda t: t[:, 0:4096].rearrange("p (k g c) -> p k g c", k=8, g=2)[:, :, 1, :],
                           wu[:, :, DFF + 256 * u:DFF + 256 * u + 256])], [wbuf["up", f, l].r()])
            sv = ring_t[s][:, 0:4096].rearrange("p (k g c) -> p k g c", k=8, g=2)
            for j in range(2):
                fc = 2 * u + j
                bg, bu = nbank(), nbank()
                for g_, bnk in ((0, bg), (1, bu)):
                    for k in range(8):
                        mm(ps[:, bnk, :n], sv[:, k, g_, j * 128:(j + 1) * 128], hb[:, k, :n], k == 0, k == 7,
                           ring_b[s].r() + hb_b.r(k), ps_b.r(bnk))
                ti = fc % 2
                act_fn(tmpA[:, ti, :n], ps[:, bg, :n], AF.Silu, ps_b.r(bg), tmpA_b.r(ti))
                dve_stt(big[:, fc, :n], tmpA[:, ti, :n], 0.5, ps[:, bu, :n], ALU.mult, ALU.mult,
                        tmpA_b.r(ti) + ps_b.r(bu), big_b.r(fc))
        for q in range(4):
            ss = []
            for half in range(2):
                ss.append(ring_get([(lambda t: t[:, 0:2816].rearrange("p (a c) -> p a c", a=11),
                                     wd[:, 11 * half:11 * half + 11, 256 * q:256 * q + 256])],
                                   [wbuf["dn", f, l].r()]))
            for j in range(2):
                dc = 2 * q + j
                b = nbank()
                for fc in range(22):
                    s = ss[fc // 11]
                    sv = ring_t[s][:, 0:2816].rearrange("p (a c) -> p a c", a=11)
                    mm(ps[:, b, :n], sv[:, fc % 11, j * 128:(j + 1) * 128], big[:, fc, :n], fc == 0, fc == 21,
                       ring_b[s].r() + big_b.r(fc), ps_b.r(b))
                ln_flush(n)
                dve_stt(rr[:, dc, :n], hf[:, dc, :n], ALPHA, ps[:, b, :n], ALU.mult, ALU.add,
                        hf_b.r(dc) + ps_b.r(b), rr_b.r(dc))
                ln_feed(dc, n)
        layernorm(l, 0 if f == 1 else 2, n)

    def rms_rstd(bank, nchunk, feat, n):
        S.op("dve", lambda e: e.tensor_scalar(out=sdv[:, :n], in0=ps[:, bank, :n], scalar1=1.0 / feat,
                                              scalar2=RMS_EPS, op0=ALU.mult, op1=ALU.add), ps_b.r(bank), sdv_b.r())
        act_fn(sdv[:, :n], sdv[:, :n], AF.Sqrt, sdv_b.r(), sdv_b.r())
        dve_recip(rstd[:, :n], sdv[:, :n], sdv_b.r(), rstd_b.r())

    def mixer(l, n, ti):
        wi = wb["in", l].rearrange("(kc p) c -> p kc c", p=128)
        ucols = [(0, 512), (512, 1024), (1024, 1536), (1536, 1952), (1952, 2464), (2464, 2976), (2976, 3488),
                 (3488, 4000)]

        def get_in(U):
            c0, c1 = ucols[U]
            w = c1 - c0
            s = ring_get([(lambda t: t[:, 0:8 * w].rearrange("p (k c) -> p k c", k=8), wi[:, :, c0:c1])],
                         [wbuf["in", l].r()])
            return s, ring_t[s][:, 0:8 * w].rearrange("p (k c) -> p k c", k=8)

        def proj8(sv, s, col, bank, M=128):
            for k in range(8):
                mm(ps[0:M, bank, :n], sv[:, k, col:col + M], hb[:, k, :n], k == 0, k == 7,
                   ring_b[s].r() + hb_b.r(k), ps_b.r(bank))

        S.op("dve", lambda e: e.tensor_copy(out=ub[:, :, 0:2], in_=cc[:, l, :, :]), cc_b.r(l), ub_b.rs())
        s1, v1 = get_in(1)
        for c in range(4):
            b = nbank()
            proj8(v1, s1, c * 128, b)
            act_copy(csb[:, c, :n], ps[:, b, :n], ps_b.r(b), rr_b.r(c))
        s2, v2 = get_in(2)
        for c in range(4):
            b = nbank()
            proj8(v2, s2, c * 128, b)
            dve_tt(ub[:, c, 2:2 + n], csb[:, c, :n], ps[:, b, :n], ALU.mult, rr_b.r(c) + ps_b.r(b), ub_b.r(c))
        S.op("dve", lambda e: e.tensor_copy(out=cc[:, l, :, :], in_=ub[:, :, n:n + 2]), ub_b.rs(), cc_b.r(l))
        s0, v0 = get_in(0)
        for c in range(4):
            t_ = c % 2
            ci = (l * 3) * 4 + c
            pool_ts(tmpA[:, t_, :n], ub[:, c, 2:2 + n], cwt[:, ci + 8:ci + 9], ALU.mult,
                    ub_b.r(c) + par_b.r(), tmpA_b.r(t_))
            pool_ts(tmpM[:, t_, :n], ub[:, c, 1:1 + n], cwt[:, ci + 4:ci + 5], ALU.mult,
                    ub_b.r(c) + par_b.r(), tmpM_b.r(t_))
            pool_tt(tmpA[:, t_, :n], tmpA[:, t_, :n], tmpM[:, t_, :n], ALU.add, tmpA_b.r(t_) + tmpM_b.r(t_),
                    tmpA_b.r(t_))
            pool_ts(tmpM[:, t_, :n], ub[:, c, 0:n], cwt[:, ci:ci + 1], ALU.mult,
                    ub_b.r(c) + par_b.r(), tmpM_b.r(t_))
            pool_tt(tmpA[:, t_, :n], tmpA[:, t_, :n], tmpM[:, t_, :n], ALU.add, tmpA_b.r(t_) + tmpM_b.r(t_),
                    tmpA_b.r(t_))
            b = nbank()
            proj8(v0, s0, c * 128, b)
            dve_tt(ycb[:, c, :n], tmpA[:, t_, :n], ps[:, b, :n], ALU.mult, tmpA_b.r(t_) + ps_b.r(b), ycb_b.r(c))

        s3, v3 = get_in(3)
        skv = ring_get([(lambda t: t[:, 0:1024], wb["ukv", l][:, :]),
                        (lambda t: t[:, 1024:2560].rearrange("p (k c) -> p k c", k=8),
                         wb["kr", l].rearrange("(kc p) a c -> p kc (a c)", p=128))],
                       [wbuf["ukv", l].r(), wbuf["kr", l].r()])
        wukv = ring_t[skv][:, 0:1024]
        wkr = ring_t[skv][:, 1024:2560].rearrange("p (k a c) -> p k a c", k=8, a=2)
        suq = ring_get([(lambda t: t[:, 0:1024].rearrange("p (k c) -> p k c", k=2),
                         wb["uqN", l].rearrange("(kc p) j c -> p kc (j c)", p=128)),
                        (lambda t: t[:, 1024:2048].rearrange("p (k c) -> p k c", k=2),
                         wb["uqR", l].rearrange("(kc p) a g c -> p kc (a g c)", p=128))],
                       [wbuf["uqN", l].r(), wbuf["uqR", l].r()])
        wuqN = ring_t[suq][:, 0:1024].rearrange("p (k j c) -> p k j c", k=2, j=4)
        wuqR = ring_t[suq][:, 1024:2048].rearrange("p (k a g c) -> p k a g c", k=2, a=2, g=2)
        bkv = nbank()
        proj8(v3, s3, 256, bkv)
        act_fn(rb[:, 2, :n], ps[:, bkv, :n], AF.Square, ps_b.r(bkv), rb_b.r(2))
        bs = nbank()
        mm(ps[:, bs, :n], ones[:, :], rb[:, 2, :n], True, True, rb_b.r(2) + par_b.r(), ps_b.r(bs))
        rms_rstd(bs, 1, 128.0, n)
        dve_stt(ckvn[:, :n], ps[:, bkv, :n], gkvt[:, l:l + 1], rstd[:, :n], ALU.mult, ALU.mult,
                ps_b.r(bkv) + rstd_b.r() + par_b.r(), big_b.r(20))
        if ti == 0:
            def ktv(h, a, b_): return mKT[a:b_, l, h, :n]
            kt_w = lambda h: mKT_b.r(l)
        else:
            def ktv(h, a, b_): return big[a:b_, 8 + h, :n]
            kt_w = lambda h: big_b.r(8 + h)
        for h in range(NH):
            b = nbank()
            mm(ps[:, b, :n], wukv[:, h * 128:(h + 1) * 128], ckvn[:, :n], True, True,
               ring_b[skv].r() + big_b.r(20), ps_b.r(b))
            act_copy(ktv(h, 0, 64), ps[0:64, b, :n], ps_b.r(b), kt_w(h))
        wv = wukv.rearrange("p (h c) -> p h c", h=NH)[:, :, 64:128]
        nblk = (n + 127) // 128
        for tb in range(nblk):
            m = min(128, n - tb * 128)
            b = nbank()
            mm(ps[0:m, b, :].rearrange("p (h c) -> p h c", h=NH), ckvn[:, tb * 128:tb * 128 + m], wv, True, True,
               ring_b[skv].r() + big_b.r(20), ps_b.r(b))
            src = ps[0:m, b, :].rearrange("p (h c) -> p h c", h=NH)
            if ti == 0:
                act_copy(mV[0:m, l, :, 0:64], src, ps_b.r(b), mV_b.r(l))
            else:
                act_copy(Vt[0:m, tb, :, 0:64], src, ps_b.r(b), Vt_b.r(tb))
        bA, bB = nbank(), nbank()
        for a_, bnk in ((0, bA), (1, bB)):
            for k in range(8):
                mm(ps[0:96, bnk, :n], wkr[:, k, a_, :], hb[:, k, :n], k == 0, k == 7,
                   ring_b[skv].r() + hb_b.r(k), ps_b.r(bnk))
        dve_tt(tmpA[64:96, 0, :n], ps[64:96, bA, :n], tabC[64:96, :n], ALU.mult, ps_b.r(bA) + tab_b.r(), tmpA_b.r(0))
        dve_tt(tmpA[64:96, 1, :n], ps[64:96, bB, :n], tabS[64:96, :n], ALU.mult, ps_b.r(bB) + tab_b.r(), tmpA_b.r(1))
        for h in range(NH):
            dve_tt(ktv(h, 64, 96), tmpA[64:96, 0, :n], tmpA[64:96, 1, :n], ALU.add,
                   tmpA_b.r(0) + tmpA_b.r(1), kt_w(h))
        if ti > 0 and ti < NT:
            t0 = (ti - 1) * TN
            dma("pool", KTd[l, :, :, t0:t0 + TN], big[0:96, 8:16, :], big_b.rs(8, 16), KTd_b.r(l * NT + ti - 1), sem_kst)
            dma("pool", Vd[l, ti - 1, :, :], Vt[:, :, :, :].rearrange("p a h c -> p (a h c)"), Vt_b.rs(),
                Vd_b.r(l * NT + ti - 1), sem_vst)
        bq = [nbank(), nbank()]
        for k in range(2):
            proj8(v3, s3, k * 128, bq[k])
            act_fn(rb[:, k, :n], ps[:, bq[k], :n], AF.Square, ps_b.r(bq[k]), rb_b.r(k))
        bs = nbank()
        for k in range(2):
            mm(ps[:, bs, :n], ones[:, :], rb[:, k, :n], k == 0, k == 1, rb_b.r(k) + par_b.r(), ps_b.r(bs))
        rms_rstd(bs, 2, 256.0, n)
        for k in range(2):
            dve_stt(cqn[:, k, :n], ps[:, bq[k], :n], gqt[:, l * 2 + k:l * 2 + k + 1], rstd[:, :n], ALU.mult, ALU.mult,
                    ps_b.r(bq[k]) + rstd_b.r() + par_b.r(), cqn_b.r(k))
        for j in range(4):
            b = nbank()
            for k in range(2):
                mm(ps[:, b, :n], wuqN[:, k, j, :], cqn[:, k, :n], k == 0, k == 1,
                   ring_b[suq].r() + cqn_b.r(k), ps_b.r(b))
            act_copy(big[0:64, 2 * j, :n], ps[0:64, b, :n], ps_b.r(b), big_b.r(2 * j))
            act_copy(big[0:64, 2 * j + 1, :n], ps[64:128, b, :n], ps_b.r(b), big_b.r(2 * j + 1))
        for g in range(2):
            bA, bB = nbank(), nbank()
            for a_, bnk in ((0, bA), (1, bB)):
                for k in range(2):
                    mm(ps[:, bnk, :n], wuqR[:, k, a_, g, :], cqn[:, k, :n], k == 0, k == 1,
                       ring_b[suq].r() + cqn_b.r(k), ps_b.r(bnk))
            dve_tt(tmpA[:, 0, :n], ps[:, bA, :n], tabC[:, :n], ALU.mult, ps_b.r(bA) + tab_b.r(), tmpA_b.r(0))
            dve_tt(tmpA[:, 1, :n], ps[:, bB, :n], tabS[:, :n], ALU.mult, ps_b.r(bB) + tab_b.r(), tmpA_b.r(1))
            for hh in range(4):
                h = 4 * g + hh
                dve_tt(big[64:96, h, :n], tmpA[32 * hh:32 * hh + 32, 0, :n], tmpA[32 * hh:32 * hh + 32, 1, :n],
                       ALU.add, tmpA_b.r(0) + tmpA_b.r(1), big_b.r(h))

        for g in range(2):
            blocks = []
            if ti == 0:
                blocks.append((lambda hh: mKT[0:96, l, 4 * g + hh, 0:n], lambda hh: mV[0:n, l, 4 * g + hh, :],
                               n, 0, True, mKT_b.r(l) + mV_b.r(l)))
            else:
                blocks.append((lambda hh: mKT[0:96, l, 4 * g + hh, :], lambda hh: mV[0:NMETA, l, 4 * g + hh, :],
                               NMETA, 0, False, mKT_b.r(l) + mV_b.r(l)))
                for jt in range(1, ti):
                    s = ring_get([(lambda t: t[0:96, 0:2048].rearrange("p (h c) -> p h c", h=4),
                                   KTd[l, :, 4 * g:4 * g + 4, (jt - 1) * TN:jt * TN]),
                                  (lambda t: t[:, 2048:4096].rearrange("p (a c) -> p a c", a=4),
                                   Vd[l, jt - 1].rearrange("p (a h c) -> p a h c", a=4, h=NH)[:, :, 4 * g:4 * g + 4, :]
                                   .rearrange("p a h c -> p a (h c)"))],
                                 [KTd_b.r(l * NT + jt - 1), Vd_b.r(l * NT + jt - 1)])
                    kv_ = ring_t[s][0:96, 0:2048].rearrange("p (h c) -> p h c", h=4)
                    vv_ = ring_t[s][:, 2048:4096].rearrange("p (a h c) -> p a h c", a=4, h=4)
                    for kb in range(4):
                        blocks.append((lambda hh, kv_=kv_, kb=kb: kv_[:, hh, kb * 128:(kb + 1) * 128],
                                       lambda hh, vv_=vv_, kb=kb: vv_[:, kb, hh, :], 128, 0, False, ring_b[s].r()))
                for kb in range(4):
                    blocks.append((lambda hh, kb=kb: big[0:96, 8 + 4 * g + hh, kb * 128:(kb + 1) * 128],
                                   lambda hh, kb=kb: Vt[:, kb, 4 * g + hh, :], 128, kb * 128, True,
                                   big_b.rs(8 + 4 * g, 12 + 4 * g) + Vt_b.r(kb)))
            items = [(bi, hh) for bi in range(len(blocks)) for hh in range(4)]
            nb = len(blocks)

            def emit_S(idx):
                bi, hh = items[idx]
                kf, vf, K, q0, msk, rd = blocks[bi]
                sbk = idx % 4
                mm(ps[0:K, sbk, q0:n], kf(hh), big[0:96, 4 * g + hh, q0:n], True, True,
                   rd + big_b.r(4 * g + hh), ps_b.r(sbk), signal=True)
                act_fn(PT[0:K, sbk, q0:n], ps[0:K, sbk, q0:n], AF.Exp, ps_b.r(sbk), PT_b.r(sbk), scale=SCALE)
                if msk:
                    dve_tt(PT[0:K, sbk, q0:q0 + K], PT[0:K, sbk, q0:q0 + K], tri[0:K, 0:K], ALU.mult,
                           PT_b.r(sbk) + par_b.r(), PT_b.r(sbk))

            def emit_PV(idx):
                bi, hh = items[idx]
                kf, vf, K, q0, msk, rd = blocks[bi]
                sbk = idx % 4
                mm(ps[:, 4 + hh, q0:n], vf(hh), PT[0:K, sbk, q0:n], bi == 0, bi == nb - 1,
                   rd + PT_b.r(sbk), ps_b.r(4 + hh), signal=True)

            LOOK = 2
            for idx in range(len(items) + LOOK):
                if idx < len(items):
                    emit_S(idx)
                if idx >= LOOK:
                    emit_PV(idx - LOOK)
            for hh in range(4):
                h = 4 * g + hh
                dve_recip(rec[64:128, :n], ps[64:128, 4 + hh, :n], ps_b.r(4 + hh), rec_b.r())
                po = (h % 2) * 64
                dve_tt(big[po:po + 64, 16 + h // 2, :n], ps[0:64, 4 + hh, :n], rec[64:128, :n], ALU.mult,
                       ps_b.r(4 + hh) + rec_b.r(), big_b.r(16 + h // 2))

        sbrc = sbrm = None
        gs = {}
        for dc in range(8):
            if dc % 4 == 0:
                gs["c"] = get_in(4 + dc // 4)
                gs["m"] = get_in(6 + dc // 4)
            if dc == 0:
                sbrc = ring_get([(lambda t: t[:, 0:4096].rearrange("p (k c) -> p k c", k=4),
                                  wb["brc", l].rearrange("(kc p) c -> p kc c", p=128))], [wbuf["brc", l].r()])
                sbrm = ring_get([(lambda t: t[:, 0:4096].rearrange("p (k c) -> p k c", k=4),
                                  wb["brm", l].rearrange("(kc p) c -> p kc c", p=128))], [wbuf["brm", l].r()])
            col = (dc % 4) * 128
            t_ = dc % 2
            b1 = nbank()
            proj8(gs["c"][1], gs["c"][0], col, b1)
            act_fn(tmpG[:, 0, :n], ps[:, b1, :n], AF.Sigmoid, ps_b.r(b1) + par_b.r(), tmpG_b.r(0),
                   bias=bgt[:, (l * 2) * 8 + dc:(l * 2) * 8 + dc + 1])
            b2 = nbank()
            proj8(gs["m"][1], gs["m"][0], col, b2)
            act_fn(tmpG[:, 1, :n], ps[:, b2, :n], AF.Sigmoid, ps_b.r(b2) + par_b.r(), tmpG_b.r(1),
                   bias=bgt[:, (l * 2 + 1) * 8 + dc:(l * 2 + 1) * 8 + dc + 1])
            b3 = nbank()
            vc = ring_t[sbrc][:, 0:4096].rearrange("p (k c) -> p k c", k=4)
            for k in range(4):
                mm(ps[:, b3, :n], vc[:, k, dc * 128:(dc + 1) * 128], ycb[:, k, :n], k == 0, k == 3,
                   ring_b[sbrc].r() + ycb_b.r(k), ps_b.r(b3))
            b4 = nbank()
            vm = ring_t[sbrm][:, 0:4096].rearrange("p (k c) -> p k c", k=4)
            for k in range(4):
                mm(ps[:, b4, :n], vm[:, k, dc * 128:(dc + 1) * 128], big[:, 16 + k, :n], k == 0, k == 3,
                   ring_b[sbrm].r() + big_b.r(16 + k), ps_b.r(b4))
            dve_tt(tmpM[:, 0, :n], tmpG[:, 0, :n], ps[:, b3, :n], ALU.mult, tmpG_b.r(0) + ps_b.r(b3), tmpM_b.r(0))
            dve_tt(tmpM[:, 1, :n], tmpG[:, 1, :n], ps[:, b4, :n], ALU.mult, tmpG_b.r(1) + ps_b.r(b4), tmpM_b.r(1))
            pool_tt(rb[:, dc, :n], tmpM[:, 0, :n], tmpM[:, 1, :n], ALU.add, tmpM_b.r(0) + tmpM_b.r(1), rb_b.r(dc))
        wo_v = wb["wo", l].rearrange("(kc p) c -> p kc c", p=128)
        for half in range(2):
            s = ring_get([(lambda t: t[:, 0:4096].rearrange("p (k c) -> p k c", k=8),
                           wo_v[:, :, 512 * half:512 * half + 512])], [wbuf["wo", l].r()])
            sv = ring_t[s][:, 0:4096].rearrange("p (k c) -> p k c", k=8)
            for j in range(4):
                dc = 4 * half + j
                b = nbank()
                for k in range(8):
                    mm(ps[:, b, :n], sv[:, k, j * 128:(j + 1) * 128], rb[:, k, :n], k == 0, k == 7,
                       ring_b[s].r() + rb_b.r(k), ps_b.r(b))
                ln_flush(n)
                dve_stt(rr[:, dc, :n], hf[:, dc, :n], ALPHA, ps[:, b, :n], ALU.mult, ALU.add,
                        hf_b.r(dc) + ps_b.r(b), rr_b.r(dc))
                ln_feed(dc, n)
        layernorm(l, 1, n)

    for ti in range(NT + 1):
        n = NMETA if ti == 0 else TN
        if ti == 0:
            dma("pool", hf[:, :, :n], metaT.rearrange("(c p) t -> p c t", p=128), [], hf_b.rs(), sem_x)
            p0 = 0
        else:
            dma("pool", hf[:, :, :n], xT[:, (ti - 1) * TN:ti * TN].rearrange("(c p) t -> p c t", p=128), [],
                hf_b.rs(), sem_x)
            p0 = NMETA + (ti - 1) * TN
        dma("pool", tabC[:, :n], ropeC[:, p0:p0 + n], [], tab_b.r(), sem_tab)
        dma("pool", tabS[:, :n], ropeS[:, p0:p0 + n], [], tab_b.r(), sem_tab)
        for dc in range(8):
            pool_copy(hb[:, dc, :n], hf[:, dc, :n], hf_b.r(dc), hb_b.r(dc))
        for l in range(L):
            ffn(l, 1, n)
            mixer(l, n, ti)
            ffn(l, 2, n)
        if ti > 0:
            dma("pool", outT[:, (ti - 1) * TN:ti * TN].rearrange("(c p) t -> p c t", p=128), hf[:, :, :], hf_b.rs(), [],
                sem_out)
    S.wait_all("pool", [sem_out, sem_kst, sem_vst])

    with nc.Block() as block:
        @block.tensor
        def _(e):
            S.replay("pe", e)

        @block.scalar
        def _(e):
            S.replay("act", e)

        @block.vector
        def _(e):
            S.replay("dve", e)

        @block.gpsimd
        def _(e):
            S.replay("pool", e)

        @block.sync
        def _(e):
            S.replay("sp", e)
    es.close()
    return nc


def _consts(seq_total):
    inv = 1.0 / (10000.0 ** (np.arange(0, 32, 2, dtype=np.float32) / 32.0))
    ang = np.arange(seq_total, dtype=np.float32)[:, None] * inv[None, :].astype(np.float32)
    cos = np.cos(ang).astype(np.float32).T
    sin = np.sin(ang).astype(np.float32).T
    C = np.tile(np.concatenate([cos, cos], 0), (4, 1))
    S_ = np.tile(np.concatenate([-sin, sin], 0), (4, 1))
    k = np.arange(128)
    tri = (k[:, None] <= k[None, :]).astype(np.float32).astype(ml_dtypes.bfloat16)
    return np.ascontiguousarray(C), np.ascontiguousarray(S_), tri


def _fm(a, nch):
    a = np.asarray(a, np.float32)
    lead = int(np.prod(a.shape[:-1]))
    return np.ascontiguousarray(a.reshape(lead, nch, 128).transpose(2, 0, 1).reshape(128, lead * nch))


def make_in_maps(inputs, NT, L, n_cores=8):
    x = np.asarray(inputs["x"], np.float32)
    B = x.shape[0]
    C, S_, tri = _consts(NMETA + NT * TN)
    shared = {k: np.ascontiguousarray(np.asarray(inputs[k], np.float32)) for k in
              ("ffn1_w_up", "ffn1_w_down", "mix_w_in", "w_uq", "w_ukv", "w_br_conv", "w_br_mla", "w_o", "ffn2_w_up",
               "ffn2_w_down")}
    shared["metaT"] = np.ascontiguousarray(np.asarray(inputs["meta_tokens"], np.float32).T)
    shared["p_lng"] = _fm(inputs["ln_g"], 8)
    shared["p_lnb"] = _fm(inputs["ln_b"], 8)
    shared["p_bg"] = _fm(inputs["mix_b_gate"], 8)
    shared["p_cw"] = _fm(inputs["conv_w"], 4)
    shared["p_gq"] = _fm(inputs["q_norm_g"], 2)
    shared["p_gkv"] = _fm(inputs["kv_norm_g"], 1)
    shared["ropeC"] = C
    shared["ropeS"] = S_
    shared["tri"] = tri
    maps = []
    for c in range(n_cores):
        m = dict(shared)
        m["xT"] = np.ascontiguousarray(x[c % B].T)
        maps.append(m)
    return maps


def kernel(**inputs):
    x = np.asarray(inputs["x"])
    B, SEQ, _ = x.shape
    NT = SEQ // TN
    L = np.asarray(inputs["ln_g"]).shape[0]
    nc = build_program(NT, L)
    maps = make_in_maps(inputs, NT, L, 8)
    res = run_bass_kernel_spmd(nc, maps, core_ids=list(range(8)))
    out = np.stack([np.ascontiguousarray(res.results[b]["outT"].T) for b in range(B)], 0)
    return out.astype(np.float32)
```

```python
import contextlib
import numpy as np
import ml_dtypes
import concourse.bass as bass
import concourse.mybir as mybir
from concourse.bass_utils import run_bass_kernel_spmd

F32 = mybir.dt.float32
BF16 = mybir.dt.bfloat16
AF = mybir.ActivationFunctionType
ALU = mybir.AluOpType

D = 1024
DFF = 2816
NMETA = 16
DIN = 4000
NH = 8
ALPHA = 4.0 ** 0.25
LN_EPS = 1e-5
RMS_EPS = 1e-6
SCALE = 96.0 ** -0.5
TN = 512
NSLOT = 7
REAL_CORES = [0, 1, 2, 3]


class Sem:
    def __init__(self, h):
        self.h = h
        self.count = 0


class Region:
    __slots__ = ("lw", "rd")

    def __init__(self):
        self.lw = None
        self.rd = {}


class Buf:
    def __init__(self, nreg=1):
        self.regs = [Region() for _ in range(nreg)]

    def r(self, k=0):
        return [self.regs[k]]

    def rs(self, a=None, b=None):
        return self.regs[a:b]


class Eng:
    def __init__(self, name, sem):
        self.name = name
        self.sem = sem
        self.items = []
        self.known = {}


class Sched:
    def __init__(self):
        self.engs = {}

    def op(self, en, fn, reads=(), writes=(), signal=True, dsem=None):
        eng = self.engs[en]
        deps = {}

        def add(d):
            if d is not None:
                s, v = d
                if deps.get(s, 0) < v:
                    deps[s] = v

        for r in reads:
            add(r.lw)
        for w in writes:
            add(w.lw)
            for s, v in w.rd.items():
                add((s, v))
        for s, v in deps.items():
            if en == "pe" and s is eng.sem:
                continue
            if eng.known.get(s, 0) >= v:
                continue
            eng.items.append(("w", s, v))
            eng.known[s] = v
        if dsem is not None:
            dsem.count += 16
            tick = (dsem, dsem.count)
            eng.items.append(("d", fn, dsem))
        elif signal:
            eng.sem.count += 1
            tick = (eng.sem, eng.sem.count)
            eng.items.append(("o", fn, True))
        else:
            tick = (eng.sem, eng.sem.count + 1)
            eng.items.append(("o", fn, False))
        for r in reads:
            if r.rd.get(tick[0], 0) < tick[1]:
                r.rd[tick[0]] = tick[1]
        for w in writes:
            w.lw = tick
            w.rd = {}

    def wait_all(self, en, sems):
        eng = self.engs[en]
        for s in sems:
            if s.count > 0:
                eng.items.append(("w", s, s.count))

    def replay(self, en, e):
        for it in self.engs[en].items:
            if it[0] == "w":
                e.wait_ge(it[1].h, it[2])
            elif it[0] == "d":
                it[1](e).then_inc(it[2].h, 16)
            else:
                ins = it[1](e)
                if it[2]:
                    ins.then_inc(self.engs[en].sem.h, 1)


def build_program(NT, L=2):
    SEQ = NT * TN
    nc = bass.Bass("TRN2", target_bir_lowering=False)
    S = Sched()
    es = contextlib.ExitStack()

    def dram(name, shape, dt, kind="Internal"):
        return nc.dram_tensor(name, list(shape), dt, kind=kind)

    xT = dram("xT", [D, SEQ], F32, "ExternalInput").ap()
    metaT = dram("metaT", [D, NMETA], F32, "ExternalInput").ap()
    w_ext = {}
    for nm, shp in (("ffn1_w_up", [L, D, 2 * DFF]), ("ffn1_w_down", [L, DFF, D]), ("mix_w_in", [L, D, DIN]),
                    ("w_uq", [L, 256, 768]), ("w_ukv", [L, 128, 1024]), ("w_br_conv", [L, 512, D]),
                    ("w_br_mla", [L, 512, D]), ("w_o", [L, D, D]), ("ffn2_w_up", [L, D, 2 * DFF]),
                    ("ffn2_w_down", [L, DFF, D])):
        w_ext[nm] = dram(nm, shp, F32, "ExternalInput").ap()
    p_lng = dram("p_lng", [128, L * 3 * 8], F32, "ExternalInput").ap()
    p_lnb = dram("p_lnb", [128, L * 3 * 8], F32, "ExternalInput").ap()
    p_bg = dram("p_bg", [128, L * 2 * 8], F32, "ExternalInput").ap()
    p_cw = dram("p_cw", [128, L * 3 * 4], F32, "ExternalInput").ap()
    p_gq = dram("p_gq", [128, L * 2], F32, "ExternalInput").ap()
    p_gkv = dram("p_gkv", [128, L], F32, "ExternalInput").ap()
    ropeC = dram("ropeC", [128, NMETA + SEQ], F32, "ExternalInput").ap()
    ropeS = dram("ropeS", [128, NMETA + SEQ], F32, "ExternalInput").ap()
    tri_d = dram("tri", [128, 128], BF16, "ExternalInput").ap()
    outT = dram("outT", [D, SEQ], F32, "ExternalOutput").ap()

    wb = {}
    for l in range(L):
        for f in (1, 2):
            wb["up", f, l] = dram(f"b_up{f}_{l}", [D, 2 * DFF], BF16).ap()
            wb["dn", f, l] = dram(f"b_dn{f}_{l}", [DFF, D], BF16).ap()
        wb["in", l] = dram(f"b_in_{l}", [D, DIN], BF16).ap()
        wb["kr", l] = dram(f"b_kr_{l}", [D, 2, 96], BF16).ap()
        wb["uqN", l] = dram(f"b_uqN_{l}", [256, 4, 128], BF16).ap()
        wb["uqR", l] = dram(f"b_uqR_{l}", [256, 2, 2, 128], BF16).ap()
        wb["ukv", l] = dram(f"b_ukv_{l}", [128, 1024], BF16).ap()
        wb["brc", l] = dram(f"b_brc_{l}", [512, D], BF16).ap()
        wb["brm", l] = dram(f"b_brm_{l}", [512, D], BF16).ap()
        wb["wo", l] = dram(f"b_wo_{l}", [D, D], BF16).ap()
    KTd = dram("KTd", [L, 96, NH, SEQ], BF16).ap()
    Vd = dram("Vd", [L, NT, 128, 4 * NH * 128], BF16).ap()
    wbuf = {k: Buf() for k in wb}
    KTd_b = Buf(L * NT)
    Vd_b = Buf(L * NT)

    def sb(name, shape, dt):
        return es.enter_context(nc.sbuf_tensor(name, list(shape), dt))

    ring_t = [sb(f"ring{i}", [128, 4096], BF16) for i in range(NSLOT)]
    ring_b = [Buf() for _ in range(NSLOT)]
    hf = sb("hf", [128, 8, TN], F32); hf_b = Buf(8)
    hb = sb("hb", [128, 8, TN], BF16); hb_b = Buf(8)
    rr = sb("rr", [128, 8, TN], F32); rr_b = Buf(8)
    big = sb("big", [128, 22, TN], BF16); big_b = Buf(22)
    rb = sb("rb", [128, 8, TN], BF16); rb_b = Buf(8)
    tmpA = sb("tmpA", [128, 2, TN], F32); tmpA_b = Buf(2)
    tmpG = sb("tmpG", [128, 2, TN], F32); tmpG_b = Buf(2)
    tmpM = sb("tmpM", [128, 2, TN], F32); tmpM_b = Buf(2)
    sqb = sb("sqb", [128, 8, TN], BF16); sqb_b = Buf(8)
    rbl = sb("rbl", [128, 8, TN], BF16); rbl_b = Buf(8)
    lnm = sb("lnm", [128, TN], F32); lnm_b = Buf()
    lnq = sb("lnq", [128, TN], F32); lnq_b = Buf()
    rstd = sb("rstd", [128, TN], F32); rstd_b = Buf()
    sdv = sb("sdv", [128, TN], F32); sdv_b = Buf()
    ub = sb("ub", [128, 4, TN + 2], F32); ub_b = Buf(4)
    ycb = sb("ycb", [128, 4, TN], BF16); ycb_b = Buf(4)
    cqn = sb("cqn", [128, 2, TN], BF16); cqn_b = Buf(2)
    Vt = sb("Vt", [128, 4, NH, 128], BF16); Vt_b = Buf(4)
    tabC = sb("tabC", [128, TN], F32); tabS = sb("tabS", [128, TN], F32); tab_b = Buf()
    PT = sb("PT", [128, 4, TN], BF16); PT_b = Buf(4)
    rec = sb("rec", [128, TN], F32); rec_b = Buf()
    cc = sb("cc", [128, L, 4, 2], F32); cc_b = Buf(L)
    mKT = sb("mKT", [128, L, NH, NMETA], BF16); mKT_b = Buf(L)
    mV = sb("mV", [128, L, NH, 128], BF16); mV_b = Buf(L)
    lng = sb("lng", [128, L * 3 * 8], F32); lnb = sb("lnb", [128, L * 3 * 8], F32)
    bgt = sb("bgt", [128, L * 2 * 8], F32); cwt = sb("cwt", [128, L * 3 * 4], F32)
    gqt = sb("gqt", [128, L * 2], F32); gkvt = sb("gkvt", [128, L], F32)
    ones = sb("ones", [128, 128], BF16); tri = sb("tri_s", [128, 128], BF16)
    par_b = Buf()
    ps = es.enter_context(nc.psum_tensor("ps", [128, 8, TN], F32)); ps_b = Buf(8)
    print("SBUF bytes remaining per partition:", nc.sbuf_bytes_remaining)

    def qT(h): return big[:, h, :]
    def KT(h): return big[:, 8 + h, :]
    def ymla(c): return big[:, 16 + c, :]
    ckvn = big[:, 20, :]
    csb = rr

    def sem(name):
        return Sem(es.enter_context(nc.semaphore(name)))

    for en in ("pe", "act", "dve", "pool", "sp"):
        S.engs[en] = Eng(en, sem("s_" + en))
    ring_sem = [sem(f"s_ring{i}") for i in range(NSLOT)]
    sem_par = sem("s_par")
    sem_x = sem("s_x")
    sem_tab = sem("s_tab")
    sem_kst = sem("s_kst")
    sem_vst = sem("s_vst")
    sem_out = sem("s_out")

    def dma(q, out, in_, reads, writes, dsem):
        S.op(q, lambda e: e.dma_start(out=out, in_=in_), reads, writes, dsem=dsem)

    def mm(out, lhsT, rhs, start, stop, reads, writes, signal=None):
        if signal is None:
            signal = stop
        S.op("pe", lambda e: e.matmul(out, lhsT=lhsT, rhs=rhs, start=start, stop=stop), reads, writes, signal=signal)

    def act_fn(out, in_, func, reads, writes, scale=1.0, bias=0.0):
        S.op("act", lambda e: e.activation(out=out, in_=in_, func=func, bias=bias, scale=scale), reads, writes)

    def act_copy(out, in_, reads, writes):
        S.op("act", lambda e: e.copy(out=out, in_=in_), reads, writes)

    def dve_tt(out, in0, in1, op, reads, writes):
        S.op("dve", lambda e: e.tensor_tensor(out=out, in0=in0, in1=in1, op=op), reads, writes)

    def dve_stt(out, in0, scalar, in1, op0, op1, reads, writes):
        S.op("dve", lambda e: e.scalar_tensor_tensor(out=out, in0=in0, scalar=scalar, in1=in1, op0=op0, op1=op1),
             reads, writes)

    def dve_ts(out, in0, scalar1, op0, reads, writes):
        S.op("dve", lambda e: e.tensor_scalar(out=out, in0=in0, scalar1=scalar1, scalar2=None, op0=op0), reads, writes)

    def pool_tt(out, in0, in1, op, reads, writes):
        S.op("pool", lambda e: e.tensor_tensor(out=out, in0=in0, in1=in1, op=op), reads, writes)

    def pool_stt(out, in0, scalar, in1, op0, op1, reads, writes):
        S.op("pool", lambda e: e.scalar_tensor_tensor(out=out, in0=in0, scalar=scalar, in1=in1, op0=op0, op1=op1),
             reads, writes)

    def pool_ts(out, in0, scalar1, op0, reads, writes):
        S.op("pool", lambda e: e.tensor_scalar(out=out, in0=in0, scalar1=scalar1, scalar2=None, op0=op0), reads, writes)

    def pool_copy(out, in_, reads, writes):
        S.op("pool", lambda e: e.tensor_copy(out=out, in_=in_), reads, writes)

    def dve_copy(out, in_, reads, writes):
        S.op("dve", lambda e: e.tensor_copy(out=out, in_=in_), reads, writes)

    def dve_recip(out, in_, reads, writes):
        S.op("dve", lambda e: e.reciprocal(out=out, in_=in_), reads, writes)

    psn = [0]

    def nbank():
        b = psn[0]
        psn[0] = (b + 1) % 6
        return b

    rn = [0]

    def ring_get(loads, src_bufs):
        s = rn[0]
        rn[0] = (s + 1) % NSLOT
        reads = [r for b in src_bufs for r in b]
        for (vf, src) in loads:
            dma("sp", vf(ring_t[s]), src, reads, ring_b[s].r(), ring_sem[s])
        return s

    for (t, d_) in ((lng, p_lng), (lnb, p_lnb), (bgt, p_bg), (cwt, p_cw), (gqt, p_gq), (gkvt, p_gkv), (tri, tri_d)):
        dma("pool", t[:, :], d_[:, :], [], par_b.r(), sem_par)
    S.op("dve", lambda e: e.memset(ones[:, :], 1.0), [], par_b.r())
    S.op("dve", lambda e: e.memset(cc[:, :, :, :], 0.0), [], cc_b.rs())
    for blk in range(4):
        S.op("dve", lambda e, blk=blk: e.memset(Vt[:, blk, :, 64:128], 1.0), [], Vt_b.r(blk))
    for l in range(L):
        S.op("dve", lambda e, l=l: e.memset(mV[:, l, :, 64:128], 1.0), [], mV_b.r(l))

    cast_sems = {}

    def cast(dst_key, dst_ap, src_ap):
        if dst_key not in cast_sems:
            cast_sems[dst_key] = sem("s_c_" + "_".join(str(x) for x in dst_key))
        dma("pool", dst_ap, src_ap, [], wbuf[dst_key].r(), cast_sems[dst_key])

    def cast_rows(dst_key, src, rows, blk=256):
        for r0 in range(0, rows, blk):
            cast(dst_key, wb[dst_key][r0:r0 + blk, :], src[r0:r0 + blk, :])

    for l in range(L):
        cast_rows(("up", 1, l), w_ext["ffn1_w_up"][l], D)
        cast_rows(("dn", 1, l), w_ext["ffn1_w_down"][l], DFF)
        win = w_ext["mix_w_in"][l]
        cast_rows(("in", l), win, D)
        cast(("kr", l), wb["kr", l][:, 0, :], win[:, 1856:1952])
        cast(("kr", l), wb["kr", l][:, 1, 0:64], win[:, 1856:1920])
        cast(("kr", l), wb["kr", l][:, 1, 64:80], win[:, 1936:1952])
        cast(("kr", l), wb["kr", l][:, 1, 80:96], win[:, 1920:1936])
        wq = w_ext["w_uq"][l].rearrange("k (h c) -> k h c", h=NH)
        for h in range(NH):
            g_, hh = h // 4, h % 4
            cast(("uqN", l), wb["uqN", l][:, h // 2, (h % 2) * 64:(h % 2) * 64 + 64], wq[:, h, 0:64])
            cast(("uqR", l), wb["uqR", l][:, 0, g_, hh * 32:hh * 32 + 32], wq[:, h, 64:96])
            cast(("uqR", l), wb["uqR", l][:, 1, g_, hh * 32:hh * 32 + 16], wq[:, h, 80:96])
            cast(("uqR", l), wb["uqR", l][:, 1, g_, hh * 32 + 16:hh * 32 + 32], wq[:, h, 64:80])
        cast(("ukv", l), wb["ukv", l][:, :], w_ext["w_ukv"][l])
        cast_rows(("brc", l), w_ext["w_br_conv"][l], 512)
        cast_rows(("brm", l), w_ext["w_br_mla"][l], 512)
        cast_rows(("wo", l), w_ext["w_o"][l], D)
        cast_rows(("up", 2, l), w_ext["ffn2_w_up"][l], D)
        cast_rows(("dn", 2, l), w_ext["ffn2_w_down"][l], DFF)

    ln_pend = []

    def ln_feed(dc, n):
        act_copy(rbl[:, dc, :n], rr[:, dc, :n], rr_b.r(dc), rbl_b.r(dc))
        act_fn(sqb[:, dc, :n], rr[:, dc, :n], AF.Square, rr_b.r(dc), sqb_b.r(dc))
        ln_pend.append(dc)

    def ln_flush(n):
        for dc in ln_pend:
            mm(ps[:, 6, :n], ones[:, :], rbl[:, dc, :n], dc == 0, dc == 7, rbl_b.r(dc) + par_b.r(), ps_b.r(6))
            mm(ps[:, 7, :n], ones[:, :], sqb[:, dc, :n], dc == 0, dc == 7, sqb_b.r(dc) + par_b.r(), ps_b.r(7))
        ln_pend.clear()

    def layernorm(l, i, n):
        ln_flush(n)
        S.op("dve", lambda e: e.tensor_scalar(out=lnm[:, :n], in0=ps[:, 6, :n], scalar1=1.0 / D, scalar2=None,
                                              op0=ALU.mult), ps_b.r(6), lnm_b.r())
        S.op("dve", lambda e: e.tensor_scalar(out=lnq[:, :n], in0=ps[:, 7, :n], scalar1=1.0 / D, scalar2=LN_EPS,
                                              op0=ALU.mult, op1=ALU.add), ps_b.r(7), lnq_b.r())
        dve_tt(sdv[:, :n], lnm[:, :n], lnm[:, :n], ALU.mult, lnm_b.r(), sdv_b.r())
        dve_tt(lnq[:, :n], lnq[:, :n], sdv[:, :n], ALU.subtract, lnq_b.r() + sdv_b.r(), lnq_b.r())
        act_fn(sdv[:, :n], lnq[:, :n], AF.Sqrt, lnq_b.r(), sdv_b.r())
        dve_recip(rstd[:, :n], sdv[:, :n], sdv_b.r(), rstd_b.r())
        dve_stt(lnm[:, :n], lnm[:, :n], -1.0, rstd[:, :n], ALU.mult, ALU.mult, lnm_b.r() + rstd_b.r(), lnm_b.r())
        for dc in range(8):
            gi = (l * 3 + i) * 8 + dc
            dve_tt(rr[:, dc, :n], rr[:, dc, :n], rstd[:, :n], ALU.mult, rr_b.r(dc) + rstd_b.r(), rr_b.r(dc))
            dve_tt(rr[:, dc, :n], rr[:, dc, :n], lnm[:, :n], ALU.add, rr_b.r(dc) + lnm_b.r(), rr_b.r(dc))
            act_fn(hb[:, dc, :n], rr[:, dc, :n], AF.Identity, rr_b.r(dc) + par_b.r(), hb_b.r(dc),
                   scale=lng[:, gi:gi + 1], bias=lnb[:, gi:gi + 1])
            act_fn(hf[:, dc, :n], rr[:, dc, :n], AF.Identity, rr_b.r(dc) + par_b.r(), hf_b.r(dc),
                   scale=lng[:, gi:gi + 1], bias=lnb[:, gi:gi + 1])

    def ffn(l, f, n):
        wu = wb["up", f, l].rearrange("(kc p) c -> p kc c", p=128)
        wd = wb["dn", f, l].rearrange("(fc p) d -> p fc d", p=128)
        for u in range(11):
            s = ring_get([(lambda t: t[:, 0:4096].rearrange("p (k g c) -> p k g c", k=8, g=2)[:, :, 0, :],
                           wu[:, :, 256 * u:256 * u + 256]),
                          (lambda t: t[:, 0:4096].rearrange("p (k g c) -> p k g c", k=8, g=2)[:, :, 1, :],
                           wu[:, :, DFF + 256 * u:DFF + 256 * u + 256])], [wbuf["up", f, l].r()])
            sv = ring_t[s][:, 0:4096].rearrange("p (k g c) -> p k g c", k=8, g=2)
            for j in range(2):
                fc = 2 * u + j
                bg, bu = nbank(), nbank()
                for g_, bnk in ((0, bg), (1, bu)):
                    for k in range(8):
                        mm(ps[:, bnk, :n], sv[:, k, g_, j * 128:(j + 1) * 128], hb[:, k, :n], k == 0, k == 7,
                           ring_b[s].r() + hb_b.r(k), ps_b.r(bnk))
                ti = fc % 2
                act_fn(tmpA[:, ti, :n], ps[:, bg, :n], AF.Silu, ps_b.r(bg), tmpA_b.r(ti))
                dve_stt(big[:, fc, :n], tmpA[:, ti, :n], 0.5, ps[:, bu, :n], ALU.mult, ALU.mult,
                        tmpA_b.r(ti) + ps_b.r(bu), big_b.r(fc))
        for q in range(4):
            ss = []
            for half in range(2):
                ss.append(ring_get([(lambda t: t[:, 0:2816].rearrange("p (a c) -> p a c", a=11),
                                     wd[:, 11 * half:11 * half + 11, 256 * q:256 * q + 256])],
                                   [wbuf["dn", f, l].r()]))
            for j in range(2):
                dc = 2 * q + j
                b = nbank()
                for fc in range(22):
                    s = ss[fc // 11]
                    sv = ring_t[s][:, 0:2816].rearrange("p (a c) -> p a c", a=11)
                    mm(ps[:, b, :n], sv[:, fc % 11, j * 128:(j + 1) * 128], big[:, fc, :n], fc == 0, fc == 21,
                       ring_b[s].r() + big_b.r(fc), ps_b.r(b))
                ln_flush(n)
                dve_stt(rr[:, dc, :n], hf[:, dc, :n], ALPHA, ps[:, b, :n], ALU.mult, ALU.add,
                        hf_b.r(dc) + ps_b.r(b), rr_b.r(dc))
                ln_feed(dc, n)
        layernorm(l, 0 if f == 1 else 2, n)

    def rms_rstd(bank, nchunk, feat, n):
        S.op("dve", lambda e: e.tensor_scalar(out=sdv[:, :n], in0=ps[:, bank, :n], scalar1=1.0 / feat,
                                              scalar2=RMS_EPS, op0=ALU.mult, op1=ALU.add), ps_b.r(bank), sdv_b.r())
        act_fn(sdv[:, :n], sdv[:, :n], AF.Sqrt, sdv_b.r(), sdv_b.r())
        dve_recip(rstd[:, :n], sdv[:, :n], sdv_b.r(), rstd_b.r())

    def mixer(l, n, ti):
        wi = wb["in", l].rearrange("(kc p) c -> p kc c", p=128)
        ucols = [(0, 512), (512, 1024), (1024, 1536), (1536, 1952), (1952, 2464), (2464, 2976), (2976, 3488),
                 (3488, 4000)]

        def get_in(U):
            c0, c1 = ucols[U]
            w = c1 - c0
            s = ring_get([(lambda t: t[:, 0:8 * w].rearrange("p (k c) -> p k c", k=8), wi[:, :, c0:c1])],
                         [wbuf["in", l].r()])
            return s, ring_t[s][:, 0:8 * w].rearrange("p (k c) -> p k c", k=8)

        def proj8(sv, s, col, bank, M=128):
            for k in range(8):
                mm(ps[0:M, bank, :n], sv[:, k, col:col + M], hb[:, k, :n], k == 0, k == 7,
                   ring_b[s].r() + hb_b.r(k), ps_b.r(bank))

        S.op("dve", lambda e: e.tensor_copy(out=ub[:, :, 0:2], in_=cc[:, l, :, :]), cc_b.r(l), ub_b.rs())
        s1, v1 = get_in(1)
        for c in range(4):
            b = nbank()
            proj8(v1, s1, c * 128, b)
            act_copy(csb[:, c, :n], ps[:, b, :n], ps_b.r(b), rr_b.r(c))
        s2, v2 = get_in(2)
        for c in range(4):
            b = nbank()
            proj8(v2, s2, c * 128, b)
            dve_tt(ub[:, c, 2:2 + n], csb[:, c, :n], ps[:, b, :n], ALU.mult, rr_b.r(c) + ps_b.r(b), ub_b.r(c))
        S.op("dve", lambda e: e.tensor_copy(out=cc[:, l, :, :], in_=ub[:, :, n:n + 2]), ub_b.rs(), cc_b.r(l))
        s0, v0 = get_in(0)
        for c in range(4):
            t_ = c % 2
            ci = (l * 3) * 4 + c
            dve_ts(tmpA[:, t_, :n], ub[:, c, 2:2 + n], cwt[:, ci + 8:ci + 9], ALU.mult,
                   ub_b.r(c) + par_b.r(), tmpA_b.r(t_))
            dve_stt(tmpA[:, t_, :n], ub[:, c, 1:1 + n], cwt[:, ci + 4:ci + 5], tmpA[:, t_, :n], ALU.mult, ALU.add,
                    ub_b.r(c) + tmpA_b.r(t_), tmpA_b.r(t_))
            dve_stt(tmpA[:, t_, :n], ub[:, c, 0:n], cwt[:, ci:ci + 1], tmpA[:, t_, :n], ALU.mult, ALU.add,
                    ub_b.r(c) + tmpA_b.r(t_), tmpA_b.r(t_))
            b = nbank()
            proj8(v0, s0, c * 128, b)
            dve_tt(ycb[:, c, :n], tmpA[:, t_, :n], ps[:, b, :n], ALU.mult, tmpA_b.r(t_) + ps_b.r(b), ycb_b.r(c))

        s3, v3 = get_in(3)
        skv = ring_get([(lambda t: t[:, 0:1024], wb["ukv", l][:, :]),
                        (lambda t: t[:, 1024:2560].rearrange("p (k c) -> p k c", k=8),
                         wb["kr", l].rearrange("(kc p) a c -> p kc (a c)", p=128))],
                       [wbuf["ukv", l].r(), wbuf["kr", l].r()])
        wukv = ring_t[skv][:, 0:1024]
        wkr = ring_t[skv][:, 1024:2560].rearrange("p (k a c) -> p k a c", k=8, a=2)
        suq = ring_get([(lambda t: t[:, 0:1024].rearrange("p (k c) -> p k c", k=2),
                         wb["uqN", l].rearrange("(kc p) j c -> p kc (j c)", p=128)),
                        (lambda t: t[:, 1024:2048].rearrange("p (k c) -> p k c", k=2),
                         wb["uqR", l].rearrange("(kc p) a g c -> p kc (a g c)", p=128))],
                       [wbuf["uqN", l].r(), wbuf["uqR", l].r()])
        wuqN = ring_t[suq][:, 0:1024].rearrange("p (k j c) -> p k j c", k=2, j=4)
        wuqR = ring_t[suq][:, 1024:2048].rearrange("p (k a g c) -> p k a g c", k=2, a=2, g=2)
        bkv = nbank()
        proj8(v3, s3, 256, bkv)
        act_fn(rb[:, 2, :n], ps[:, bkv, :n], AF.Square, ps_b.r(bkv), rb_b.r(2))
        bs = nbank()
        mm(ps[:, bs, :n], ones[:, :], rb[:, 2, :n], True, True, rb_b.r(2) + par_b.r(), ps_b.r(bs))
        rms_rstd(bs, 1, 128.0, n)
        dve_stt(ckvn[:, :n], ps[:, bkv, :n], gkvt[:, l:l + 1], rstd[:, :n], ALU.mult, ALU.mult,
                ps_b.r(bkv) + rstd_b.r() + par_b.r(), big_b.r(20))
        if ti == 0:
            def ktv(h, a, b_): return mKT[a:b_, l, h, :n]
            kt_w = lambda h: mKT_b.r(l)
        else:
            def ktv(h, a, b_): return big[a:b_, 8 + h, :n]
            kt_w = lambda h: big_b.r(8 + h)
        for h in range(NH):
            b = nbank()
            mm(ps[:, b, :n], wukv[:, h * 128:(h + 1) * 128], ckvn[:, :n], True, True,
               ring_b[skv].r() + big_b.r(20), ps_b.r(b))
            act_copy(ktv(h, 0, 64), ps[0:64, b, :n], ps_b.r(b), kt_w(h))
        wv = wukv.rearrange("p (h c) -> p h c", h=NH)[:, :, 64:128]
        nblk = (n + 127) // 128
        for tb in range(nblk):
            m = min(128, n - tb * 128)
            b = nbank()
            mm(ps[0:m, b, :].rearrange("p (h c) -> p h c", h=NH), ckvn[:, tb * 128:tb * 128 + m], wv, True, True,
               ring_b[skv].r() + big_b.r(20), ps_b.r(b))
            src = ps[0:m, b, :].rearrange("p (h c) -> p h c", h=NH)
            if ti == 0:
                act_copy(mV[0:m, l, :, 0:64], src, ps_b.r(b), mV_b.r(l))
            else:
                act_copy(Vt[0:m, tb, :, 0:64], src, ps_b.r(b), Vt_b.r(tb))
        bA, bB = nbank(), nbank()
        for a_, bnk in ((0, bA), (1, bB)):
            for k in range(8):
                mm(ps[0:96, bnk, :n], wkr[:, k, a_, :], hb[:, k, :n], k == 0, k == 7,
                   ring_b[skv].r() + hb_b.r(k), ps_b.r(bnk))
        dve_tt(tmpA[64:96, 0, :n], ps[64:96, bA, :n], tabC[64:96, :n], ALU.mult, ps_b.r(bA) + tab_b.r(), tmpA_b.r(0))
        dve_tt(tmpA[64:96, 1, :n], ps[64:96, bB, :n], tabS[64:96, :n], ALU.mult, ps_b.r(bB) + tab_b.r(), tmpA_b.r(1))
        for h in range(NH):
            dve_tt(ktv(h, 64, 96), tmpA[64:96, 0, :n], tmpA[64:96, 1, :n], ALU.add,
                   tmpA_b.r(0) + tmpA_b.r(1), kt_w(h))
        if ti > 0 and ti < NT:
            t0 = (ti - 1) * TN
            dma("pool", KTd[l, :, :, t0:t0 + TN], big[0:96, 8:16, :], big_b.rs(8, 16), KTd_b.r(l * NT + ti - 1), sem_kst)
            dma("pool", Vd[l, ti - 1, :, :], Vt[:, :, :, :].rearrange("p a h c -> p (a h c)"), Vt_b.rs(),
                Vd_b.r(l * NT + ti - 1), sem_vst)
        bq = [nbank(), nbank()]
        for k in range(2):
            proj8(v3, s3, k * 128, bq[k])
            act_fn(rb[:, k, :n], ps[:, bq[k], :n], AF.Square, ps_b.r(bq[k]), rb_b.r(k))
        bs = nbank()
        for k in range(2):
            mm(ps[:, bs, :n], ones[:, :], rb[:, k, :n], k == 0, k == 1, rb_b.r(k) + par_b.r(), ps_b.r(bs))
        rms_rstd(bs, 2, 256.0, n)
        for k in range(2):
            dve_stt(cqn[:, k, :n], ps[:, bq[k], :n], gqt[:, l * 2 + k:l * 2 + k + 1], rstd[:, :n], ALU.mult, ALU.mult,
                    ps_b.r(bq[k]) + rstd_b.r() + par_b.r(), cqn_b.r(k))
        for j in range(4):
            b = nbank()
            for k in range(2):
                mm(ps[:, b, :n], wuqN[:, k, j, :], cqn[:, k, :n], k == 0, k == 1,
                   ring_b[suq].r() + cqn_b.r(k), ps_b.r(b))
            act_copy(big[0:64, 2 * j, :n], ps[0:64, b, :n], ps_b.r(b), big_b.r(2 * j))
            act_copy(big[0:64, 2 * j + 1, :n], ps[64:128, b, :n], ps_b.r(b), big_b.r(2 * j + 1))
        for g in range(2):
            bA, bB = nbank(), nbank()
            for a_, bnk in ((0, bA), (1, bB)):
                for k in range(2):
                    mm(ps[:, bnk, :n], wuqR[:, k, a_, g, :], cqn[:, k, :n], k == 0, k == 1,
                       ring_b[suq].r() + cqn_b.r(k), ps_b.r(bnk))
            dve_tt(tmpA[:, 0, :n], ps[:, bA, :n], tabC[:, :n], ALU.mult, ps_b.r(bA) + tab_b.r(), tmpA_b.r(0))
            dve_tt(tmpA[:, 1, :n], ps[:, bB, :n], tabS[:, :n], ALU.mult, ps_b.r(bB) + tab_b.r(), tmpA_b.r(1))
            for hh in range(4):
                h = 4 * g + hh
                dve_tt(big[64:96, h, :n], tmpA[32 * hh:32 * hh + 32, 0, :n], tmpA[32 * hh:32 * hh + 32, 1, :n],
                       ALU.add, tmpA_b.r(0) + tmpA_b.r(1), big_b.r(h))

        for g in range(2):
            blocks = []
            if ti == 0:
                blocks.append((lambda hh: mKT[0:96, l, 4 * g + hh, 0:n], lambda hh: mV[0:n, l, 4 * g + hh, :],
                               n, 0, True, mKT_b.r(l) + mV_b.r(l)))
            else:
                blocks.append((lambda hh: mKT[0:96, l, 4 * g + hh, :], lambda hh: mV[0:NMETA, l, 4 * g + hh, :],
                               NMETA, 0, False, mKT_b.r(l) + mV_b.r(l)))
                for jt in range(1, ti):
                    s = ring_get([(lambda t: t[0:96, 0:2048].rearrange("p (h c) -> p h c", h=4),
                                   KTd[l, :, 4 * g:4 * g + 4, (jt - 1) * TN:jt * TN]),
                                  (lambda t: t[:, 2048:4096].rearrange("p (a c) -> p a c", a=4),
                                   Vd[l, jt - 1].rearrange("p (a h c) -> p a h c", a=4, h=NH)[:, :, 4 * g:4 * g + 4, :]
                                   .rearrange("p a h c -> p a (h c)"))],
                                 [KTd_b.r(l * NT + jt - 1), Vd_b.r(l * NT + jt - 1)])
                    kv_ = ring_t[s][0:96, 0:2048].rearrange("p (h c) -> p h c", h=4)
                    vv_ = ring_t[s][:, 2048:4096].rearrange("p (a h c) -> p a h c", a=4, h=4)
                    for kb in range(4):
                        blocks.append((lambda hh, kv_=kv_, kb=kb: kv_[:, hh, kb * 128:(kb + 1) * 128],
                                       lambda hh, vv_=vv_, kb=kb: vv_[:, kb, hh, :], 128, 0, False, ring_b[s].r()))
                for kb in range(4):
                    blocks.append((lambda hh, kb=kb: big[0:96, 8 + 4 * g + hh, kb * 128:(kb + 1) * 128],
                                   lambda hh, kb=kb: Vt[:, kb, 4 * g + hh, :], 128, kb * 128, True,
                                   big_b.rs(8 + 4 * g, 12 + 4 * g) + Vt_b.r(kb)))
            items = [(bi, hh) for bi in range(len(blocks)) for hh in range(4)]
            nb = len(blocks)

            def emit_S(idx):
                bi, hh = items[idx]
                kf, vf, K, q0, msk, rd = blocks[bi]
                sbk = idx % 4
                mm(ps[0:K, sbk, q0:n], kf(hh), big[0:96, 4 * g + hh, q0:n], True, True,
                   rd + big_b.r(4 * g + hh), ps_b.r(sbk), signal=True)
                act_fn(PT[0:K, sbk, q0:n], ps[0:K, sbk, q0:n], AF.Exp, ps_b.r(sbk), PT_b.r(sbk), scale=SCALE)
                if msk:
                    dve_tt(PT[0:K, sbk, q0:q0 + K], PT[0:K, sbk, q0:q0 + K], tri[0:K, 0:K], ALU.mult,
                           PT_b.r(sbk) + par_b.r(), PT_b.r(sbk))

            def emit_PV(idx):
                bi, hh = items[idx]
                kf, vf, K, q0, msk, rd = blocks[bi]
                sbk = idx % 4
                mm(ps[:, 4 + hh, q0:n], vf(hh), PT[0:K, sbk, q0:n], bi == 0, bi == nb - 1,
                   rd + PT_b.r(sbk), ps_b.r(4 + hh), signal=True)

            LOOK = 2
            for idx in range(len(items) + LOOK):
                if idx < len(items):
                    emit_S(idx)
                if idx >= LOOK:
                    emit_PV(idx - LOOK)
            for hh in range(4):
                h = 4 * g + hh
                dve_recip(rec[64:128, :n], ps[64:128, 4 + hh, :n], ps_b.r(4 + hh), rec_b.r())
                po = (h % 2) * 64
                dve_tt(big[po:po + 64, 16 + h // 2, :n], ps[0:64, 4 + hh, :n], rec[64:128, :n], ALU.mult,
                       ps_b.r(4 + hh) + rec_b.r(), big_b.r(16 + h // 2))

        sbrc = sbrm = None
        gs = {}
        for dc in range(8):
            if dc % 4 == 0:
                gs["c"] = get_in(4 + dc // 4)
                gs["m"] = get_in(6 + dc // 4)
            if dc == 0:
                sbrc = ring_get([(lambda t: t[:, 0:4096].rearrange("p (k c) -> p k c", k=4),
                                  wb["brc", l].rearrange("(kc p) c -> p kc c", p=128))], [wbuf["brc", l].r()])
                sbrm = ring_get([(lambda t: t[:, 0:4096].rearrange("p (k c) -> p k c", k=4),
                                  wb["brm", l].rearrange("(kc p) c -> p kc c", p=128))], [wbuf["brm", l].r()])
            col = (dc % 4) * 128
            t_ = dc % 2
            b1 = nbank()
            proj8(gs["c"][1], gs["c"][0], col, b1)
            act_fn(tmpG[:, 0, :n], ps[:, b1, :n], AF.Sigmoid, ps_b.r(b1) + par_b.r(), tmpG_b.r(0),
                   bias=bgt[:, (l * 2) * 8 + dc:(l * 2) * 8 + dc + 1])
            b2 = nbank()
            proj8(gs["m"][1], gs["m"][0], col, b2)
            act_fn(tmpG[:, 1, :n], ps[:, b2, :n], AF.Sigmoid, ps_b.r(b2) + par_b.r(), tmpG_b.r(1),
                   bias=bgt[:, (l * 2 + 1) * 8 + dc:(l * 2 + 1) * 8 + dc + 1])
            b3 = nbank()
            vc = ring_t[sbrc][:, 0:4096].rearrange("p (k c) -> p k c", k=4)
            for k in range(4):
                mm(ps[:, b3, :n], vc[:, k, dc * 128:(dc + 1) * 128], ycb[:, k, :n], k == 0, k == 3,
                   ring_b[sbrc].r() + ycb_b.r(k), ps_b.r(b3))
            b4 = nbank()
            vm = ring_t[sbrm][:, 0:4096].rearrange("p (k c) -> p k c", k=4)
            for k in range(4):
                mm(ps[:, b4, :n], vm[:, k, dc * 128:(dc + 1) * 128], big[:, 16 + k, :n], k == 0, k == 3,
                   ring_b[sbrm].r() + big_b.r(16 + k), ps_b.r(b4))
            dve_tt(tmpM[:, 0, :n], tmpG[:, 0, :n], ps[:, b3, :n], ALU.mult, tmpG_b.r(0) + ps_b.r(b3), tmpM_b.r(0))
            dve_tt(tmpM[:, 1, :n], tmpG[:, 1, :n], ps[:, b4, :n], ALU.mult, tmpG_b.r(1) + ps_b.r(b4), tmpM_b.r(1))
            dve_tt(rb[:, dc, :n], tmpM[:, 0, :n], tmpM[:, 1, :n], ALU.add, tmpM_b.r(0) + tmpM_b.r(1), rb_b.r(dc))
        wo_v = wb["wo", l].rearrange("(kc p) c -> p kc c", p=128)
        for half in range(2):
            s = ring_get([(lambda t: t[:, 0:4096].rearrange("p (k c) -> p k c", k=8),
                           wo_v[:, :, 512 * half:512 * half + 512])], [wbuf["wo", l].r()])
            sv = ring_t[s][:, 0:4096].rearrange("p (k c) -> p k c", k=8)
            for j in range(4):
                dc = 4 * half + j
                b = nbank()
                for k in range(8):
                    mm(ps[:, b, :n], sv[:, k, j * 128:(j + 1) * 128], rb[:, k, :n], k == 0, k == 7,
                       ring_b[s].r() + rb_b.r(k), ps_b.r(b))
                ln_flush(n)
                dve_stt(rr[:, dc, :n], hf[:, dc, :n], ALPHA, ps[:, b, :n], ALU.mult, ALU.add,
                        hf_b.r(dc) + ps_b.r(b), rr_b.r(dc))
                ln_feed(dc, n)
        layernorm(l, 1, n)

    for ti in range(NT + 1):
        n = NMETA if ti == 0 else TN
        if ti == 0:
            dma("pool", hf[:, :, :n], metaT.rearrange("(c p) t -> p c t", p=128), [], hf_b.rs(), sem_x)
            p0 = 0
        else:
            dma("pool", hf[:, :, :n], xT[:, (ti - 1) * TN:ti * TN].rearrange("(c p) t -> p c t", p=128), [],
                hf_b.rs(), sem_x)
            p0 = NMETA + (ti - 1) * TN
        dma("pool", tabC[:, :n], ropeC[:, p0:p0 + n], [], tab_b.r(), sem_tab)
        dma("pool", tabS[:, :n], ropeS[:, p0:p0 + n], [], tab_b.r(), sem_tab)
        for dc in range(8):
            dve_copy(hb[:, dc, :n], hf[:, dc, :n], hf_b.r(dc), hb_b.r(dc))
        for l in range(L):
            ffn(l, 1, n)
            mixer(l, n, ti)
            ffn(l, 2, n)
        if ti > 0:
            dma("pool", outT[:, (ti - 1) * TN:ti * TN].rearrange("(c p) t -> p c t", p=128), hf[:, :, :], hf_b.rs(), [],
                sem_out)
    S.wait_all("pool", [sem_out, sem_kst, sem_vst])

    with nc.Block() as block:
        @block.tensor
        def _(e):
            S.replay("pe", e)

        @block.scalar
        def _(e):
            S.replay("act", e)

        @block.vector
        def _(e):
            S.replay("dve", e)

        @block.gpsimd
        def _(e):
            S.replay("pool", e)

        @block.sync
        def _(e):
            S.replay("sp", e)
    es.close()
    return nc


def _consts(seq_total):
    inv = 1.0 / (10000.0 ** (np.arange(0, 32, 2, dtype=np.float32) / 32.0))
    ang = np.arange(seq_total, dtype=np.float32)[:, None] * inv[None, :].astype(np.float32)
    cos = np.cos(ang).astype(np.float32).T
    sin = np.sin(ang).astype(np.float32).T
    C = np.tile(np.concatenate([cos, cos], 0), (4, 1))
    S_ = np.tile(np.concatenate([-sin, sin], 0), (4, 1))
    k = np.arange(128)
    tri = (k[:, None] <= k[None, :]).astype(np.float32).astype(ml_dtypes.bfloat16)
    return np.ascontiguousarray(C), np.ascontiguousarray(S_), tri


def _fm(a, nch):
    a = np.asarray(a, np.float32)
    lead = int(np.prod(a.shape[:-1]))
    return np.ascontiguousarray(a.reshape(lead, nch, 128).transpose(2, 0, 1).reshape(128, lead * nch))


def make_in_maps(inputs, NT, L, n_cores=8):
    x = np.asarray(inputs["x"], np.float32)
    B = x.shape[0]
    C, S_, tri = _consts(NMETA + NT * TN)
    shared = {k: np.ascontiguousarray(np.asarray(inputs[k], np.float32)) for k in
              ("ffn1_w_up", "ffn1_w_down", "mix_w_in", "w_uq", "w_ukv", "w_br_conv", "w_br_mla", "w_o", "ffn2_w_up",
               "ffn2_w_down")}
    shared["metaT"] = np.ascontiguousarray(np.asarray(inputs["meta_tokens"], np.float32).T)
    shared["p_lng"] = _fm(inputs["ln_g"], 8)
    shared["p_lnb"] = _fm(inputs["ln_b"], 8)
    shared["p_bg"] = _fm(inputs["mix_b_gate"], 8)
    shared["p_cw"] = _fm(inputs["conv_w"], 4)
    shared["p_gq"] = _fm(inputs["q_norm_g"], 2)
    shared["p_gkv"] = _fm(inputs["kv_norm_g"], 1)
    shared["ropeC"] = C
    shared["ropeS"] = S_
    shared["tri"] = tri
    maps = []
    for c in range(n_cores):
        m = dict(shared)
        m["xT"] = np.ascontiguousarray(x[c % B].T)
        maps.append(m)
    return maps


def kernel(**inputs):
    x = np.asarray(inputs["x"])
    B, SEQ, _ = x.shape
    NT = SEQ // TN
    L = np.asarray(inputs["ln_g"]).shape[0]
    nc = build_program(NT, L)
    maps = make_in_maps(inputs, NT, L, 8)
    res = run_bass_kernel_spmd(nc, maps, core_ids=list(range(8)))
    out = np.stack([np.ascontiguousarray(res.results[REAL_CORES[b]]["outT"].T) for b in range(B)], 0)
    return out.astype(np.float32)
```
